# Optimizing a Trainium2 kernel written in Bass

```python
import math
import jax, jax.numpy as jnp
from jax import lax
import numpy as np

D_MODEL = 1024
BATCH = 8
SEQ = 2048
DEPTH = 4
DEC_BATCH = 128
DEC_SEQ = 1
PAST_LEN = 2048
PAGE_SIZE = 128

N_A_LAYERS = DEPTH // 2
N_B_LAYERS = DEPTH - N_A_LAYERS
E_A = D_MODEL
GROUP_CH = 16
N_GROUPS = E_A // GROUP_CH
P_STATE = 64
N_HEADS = 16
HEAD_DIM = D_MODEL // N_HEADS
E_B = N_HEADS * HEAD_DIM
Q_BLOCK = 128
SB_BIAS_INIT = -7.0
EPS = 1e-6

kernel_name = 'yoco_s5_stickbreak_decoder_step'

F32 = jnp.float32


def rms_norm(x, g):
    xf = x.astype(F32)
    y = xf * lax.rsqrt(jnp.mean(xf * xf, axis=-1, keepdims=True) + EPS)
    return y * g.astype(F32)


def ada_mod(c, w_mod, b_mod, n):
    m = jax.nn.silu(c.astype(F32)) @ w_mod.astype(F32) + b_mod.astype(F32)
    return jnp.split(m[:, None, :], n, axis=-1)


def s5_discretise(lam_re, lam_im, log_dt, bmat_re, bmat_im):
    lam = lax.complex(lam_re.astype(F32), lam_im.astype(F32))
    dt = jnp.exp(log_dt.astype(F32))[:, None]
    a_bar = jnp.exp(lam * dt)
    bmat = lax.complex(bmat_re.astype(F32), bmat_im.astype(F32))
    b_bar = ((a_bar - 1.0) / lam)[..., None] * bmat
    return a_bar, b_bar


def ssm_combine(e1, e2):
    a1, b1 = e1
    a2, b2 = e2
    return a1 * a2, a2 * b1 + b2


def s5_mixer(h, h0_re, h0_im, w_in, lam_re, lam_im, log_dt, bmat_re, bmat_im,
             cmat_re, cmat_im, d_skip, w_glu, b_glu, w_out):
    n_b, t_len, _ = h.shape
    uz = h @ w_in.astype(F32)
    u, z = uz[..., :E_A], uz[..., E_A:]
    a_bar, b_bar = s5_discretise(lam_re, lam_im, log_dt, bmat_re, bmat_im)
    ug = u.reshape(n_b, t_len, N_GROUPS, GROUP_CH)
    bu = lax.complex(jnp.einsum('btgc,gpc->btgp', ug, jnp.real(b_bar)),
                     jnp.einsum('btgc,gpc->btgp', ug, jnp.imag(b_bar)))
    h0 = lax.complex(h0_re.astype(F32), h0_im.astype(F32))
    bu = bu.at[:, 0].add(a_bar[None] * h0)
    a_seq = jnp.broadcast_to(a_bar, (1, t_len, N_GROUPS, P_STATE))
    _, hs = lax.associative_scan(ssm_combine, (a_seq, bu), axis=1)
    cmat = lax.complex(cmat_re.astype(F32), cmat_im.astype(F32))
    y = jnp.real(jnp.einsum('btgp,gcp->btgc', hs, cmat)).reshape(n_b, t_len, E_A)
    y = y + d_skip.astype(F32) * u
    y = jax.nn.gelu(y)
    y = y * jax.nn.sigmoid(y @ w_glu.astype(F32) + b_glu.astype(F32))
    y = y * jax.nn.silu(z)
    return y @ w_out.astype(F32), hs[:, -1]


def shared_kv(x, c, g_kv, w_mod_kv, b_mod_kv, w_kv):
    n_b, t_len, _ = x.shape
    shift, scale = ada_mod(c, w_mod_kv, b_mod_kv, 2)
    h = rms_norm(x, g_kv) * (1.0 + scale) + shift
    kv = h @ w_kv.astype(F32)
    k = kv[..., :E_B].reshape(n_b, t_len, N_HEADS, HEAD_DIM)
    v = kv[..., E_B:].reshape(n_b, t_len, N_HEADS, HEAD_DIM)
    return k, v


def sb_attend(q, k, v, bias, q_pos, k_pos):
    z = jnp.einsum('bqhd,bkhd->bhqk', q.astype(F32), k.astype(F32)) * (HEAD_DIM ** -0.5)
    z = z + bias.astype(F32)[None, :, None, None]
    mask = k_pos[None, :] < q_pos[:, None]
    log_1m = jnp.where(mask, jax.nn.log_sigmoid(-z), 0.0)
    log_stick = lax.cumsum(log_1m, axis=3, reverse=True) - log_1m
    w = jnp.where(mask, jnp.exp(jax.nn.log_sigmoid(z) + log_stick), 0.0)
    o = jnp.einsum('bhqk,bkhd->bqhd', w, v.astype(F32))
    return o.reshape(q.shape[0], q.shape[1], E_B)


def sb_attend_blocked(q, k, v, bias, q_start):
    t_q = q.shape[1]
    outs = []
    for t0 in range(0, t_q, Q_BLOCK):
        t1 = min(t0 + Q_BLOCK, t_q)
        k_end = q_start + t1
        q_pos = q_start + jnp.arange(t0, t1)
        k_pos = jnp.arange(k_end)
        outs.append(sb_attend(q[:, t0:t1], k[:, :k_end], v[:, :k_end], bias, q_pos, k_pos))
    return jnp.concatenate(outs, axis=1)


def sb_mixer(h, keys, vals, q_start, w_in, bias, w_out):
    n_b, t_len, _ = h.shape
    qz = h @ w_in.astype(F32)
    q = qz[..., :E_B].reshape(n_b, t_len, N_HEADS, HEAD_DIM)
    z = qz[..., E_B:]
    o = sb_attend_blocked(q, keys, vals, bias, q_start)
    return (o * jax.nn.silu(z)) @ w_out.astype(F32)


def trunk(x, c, h0_re, h0_im, past_k, past_v, q_start, p):
    x = x.astype(F32)
    ssm_re, ssm_im = [], []
    keys = vals = k_new = v_new = None
    for layer in range(DEPTH):
        if layer < N_A_LAYERS:
            i = layer
            shift, scale, gate = ada_mod(c, p['w_mod_a'][i], p['b_mod_a'][i], 3)
            h = rms_norm(x, p['g_pre_a'][i]) * (1.0 + scale) + shift
            y, h_last = s5_mixer(h, h0_re[i], h0_im[i], p['w_in_a'][i], p['lam_re'][i],
                                 p['lam_im'][i], p['log_dt'][i], p['bmat_re'][i],
                                 p['bmat_im'][i], p['cmat_re'][i], p['cmat_im'][i],
                                 p['d_skip'][i], p['w_glu'][i], p['b_glu'][i],
                                 p['w_out_a'][i])
            x = x + gate * rms_norm(y, p['g_post_a'][i])
            ssm_re.append(jnp.real(h_last))
            ssm_im.append(jnp.imag(h_last))
            if layer == N_A_LAYERS - 1:
                k_new, v_new = shared_kv(x, c, p['g_kv'], p['w_mod_kv'], p['b_mod_kv'], p['w_kv'])
                if past_k is None:
                    keys, vals = k_new, v_new
                else:
                    keys = jnp.concatenate([past_k.astype(F32), k_new], axis=1)
                    vals = jnp.concatenate([past_v.astype(F32), v_new], axis=1)
        else:
            j = layer - N_A_LAYERS
            shift, scale, gate = ada_mod(c, p['w_mod_b'][j], p['b_mod_b'][j], 3)
            h = rms_norm(x, p['g_pre_b'][j]) * (1.0 + scale) + shift
            y = sb_mixer(h, keys, vals, q_start, p['w_in_b'][j], p['sb_bias'][j],
                         p['w_out_b'][j])
            x = x + gate * rms_norm(y, p['g_post_b'][j])
    return x, jnp.stack(ssm_re), jnp.stack(ssm_im), k_new, v_new


def setup_inputs(seed: int = 0) -> dict:
    key = jax.random.key(seed)
    ks = iter(jax.random.split(key, 48))

    def nrm(shape, s):
        return s * jax.random.normal(next(ks), shape, F32)

    n_pages = PAST_LEN // PAGE_SIZE
    n_used = DEC_BATCH * n_pages
    n_phys = n_used + max(1, n_used // 4)
    perm = jax.random.permutation(next(ks), n_phys)
    page_table = perm[:n_used].reshape(DEC_BATCH, n_pages).astype(jnp.int32)

    na, nb, d = N_A_LAYERS, N_B_LAYERS, D_MODEL
    lam_im_base = jnp.broadcast_to(jnp.pi * jnp.arange(P_STATE, dtype=F32), (na, N_GROUPS, P_STATE))
    return {
        'x_prompt': nrm((BATCH, SEQ, d), 1.0),
        'x_sample': nrm((DEC_BATCH, DEC_SEQ, d), 1.0),
        'c_prompt': nrm((BATCH, d), 1.0),
        'c_sample': nrm((DEC_BATCH, d), 1.0),
        'state_ssm_re': nrm((na, DEC_BATCH, N_GROUPS, P_STATE), 0.3),
        'state_ssm_im': nrm((na, DEC_BATCH, N_GROUPS, P_STATE), 0.3),
        'cache_k': nrm((n_phys, PAGE_SIZE, N_HEADS, HEAD_DIM), 1.0),
        'cache_v': nrm((n_phys, PAGE_SIZE, N_HEADS, HEAD_DIM), 1.0),
        'page_table': page_table,
        'g_pre_a': 1.0 + nrm((na, d), 0.05),
        'g_post_a': 1.0 + nrm((na, d), 0.05),
        'w_mod_a': nrm((na, d, 3 * d), 0.5 * d ** -0.5),
        'b_mod_a': nrm((na, 3 * d), 0.02),
        'w_in_a': nrm((na, d, 2 * E_A), d ** -0.5),
        'lam_re': -0.5 + nrm((na, N_GROUPS, P_STATE), 0.01),
        'lam_im': lam_im_base + nrm((na, N_GROUPS, P_STATE), 0.01),
        'log_dt': jax.random.uniform(next(ks), (na, N_GROUPS), F32,
                                     minval=math.log(1e-3), maxval=math.log(1e-1)),
        'bmat_re': nrm((na, N_GROUPS, P_STATE, GROUP_CH), (2 * GROUP_CH) ** -0.5),
        'bmat_im': nrm((na, N_GROUPS, P_STATE, GROUP_CH), (2 * GROUP_CH) ** -0.5),
        'cmat_re': nrm((na, N_GROUPS, GROUP_CH, P_STATE), (2 * P_STATE) ** -0.5),
        'cmat_im': nrm((na, N_GROUPS, GROUP_CH, P_STATE), (2 * P_STATE) ** -0.5),
        'd_skip': nrm((na, E_A), 1.0),
        'w_glu': nrm((na, E_A, E_A), E_A ** -0.5),
        'b_glu': nrm((na, E_A), 0.02),
        'w_out_a': nrm((na, E_A, d), E_A ** -0.5),
        'g_kv': 1.0 + nrm((d,), 0.05),
        'w_mod_kv': nrm((d, 2 * d), 0.5 * d ** -0.5),
        'b_mod_kv': nrm((2 * d,), 0.02),
        'w_kv': nrm((d, 2 * E_B), d ** -0.5),
        'g_pre_b': 1.0 + nrm((nb, d), 0.05),
        'g_post_b': 1.0 + nrm((nb, d), 0.05),
        'w_mod_b': nrm((nb, d, 3 * d), 0.5 * d ** -0.5),
        'b_mod_b': nrm((nb, 3 * d), 0.02),
        'w_in_b': nrm((nb, d, 2 * E_B), d ** -0.5),
        'sb_bias': SB_BIAS_INIT + nrm((nb, N_HEADS), 0.5),
        'w_out_b': nrm((nb, E_B, d), E_B ** -0.5),
    }


def reference(x_prompt, x_sample, c_prompt, c_sample, state_ssm_re, state_ssm_im,
              cache_k, cache_v, page_table, g_pre_a, g_post_a, w_mod_a, b_mod_a, w_in_a,
              lam_re, lam_im, log_dt, bmat_re, bmat_im, cmat_re, cmat_im, d_skip, w_glu,
              b_glu, w_out_a, g_kv, w_mod_kv, b_mod_kv, w_kv, g_pre_b, g_post_b, w_mod_b,
              b_mod_b, w_in_b, sb_bias, w_out_b):
    p = {
        'g_pre_a': g_pre_a, 'g_post_a': g_post_a, 'w_mod_a': w_mod_a, 'b_mod_a': b_mod_a,
        'w_in_a': w_in_a, 'lam_re': lam_re, 'lam_im': lam_im, 'log_dt': log_dt,
        'bmat_re': bmat_re, 'bmat_im': bmat_im, 'cmat_re': cmat_re, 'cmat_im': cmat_im,
        'd_skip': d_skip, 'w_glu': w_glu, 'b_glu': b_glu, 'w_out_a': w_out_a,
        'g_kv': g_kv, 'w_mod_kv': w_mod_kv, 'b_mod_kv': b_mod_kv, 'w_kv': w_kv,
        'g_pre_b': g_pre_b, 'g_post_b': g_post_b, 'w_mod_b': w_mod_b, 'b_mod_b': b_mod_b,
        'w_in_b': w_in_b, 'sb_bias': sb_bias, 'w_out_b': w_out_b,
    }
    n_prompt = x_prompt.shape[0]
    zeros = jnp.zeros((N_A_LAYERS, n_prompt, N_GROUPS, P_STATE), F32)
    y_prompt, p_ssm_re, p_ssm_im, p_k, p_v = trunk(x_prompt, c_prompt, zeros, zeros,
                                                   None, None, 0, p)
    n_seq, n_pg = page_table.shape
    past_len = n_pg * cache_k.shape[1]
    past_k = cache_k[page_table].reshape(n_seq, past_len, N_HEADS, HEAD_DIM)
    past_v = cache_v[page_table].reshape(n_seq, past_len, N_HEADS, HEAD_DIM)
    y_sample, s_ssm_re, s_ssm_im, s_k, s_v = trunk(x_sample, c_sample, state_ssm_re,
                                                   state_ssm_im, past_k, past_v,
                                                   past_len, p)
    return (y_prompt, y_sample, p_ssm_re, p_ssm_im, p_k, p_v, s_ssm_re, s_ssm_im, s_k, s_v)
```

```python
import math
from contextlib import ExitStack

import numpy as np
import ml_dtypes
import concourse.bass as bass
import concourse.mybir as mybir
from concourse.bass_utils import run_bass_kernel_spmd

F32 = mybir.dt.float32
BF16 = mybir.dt.bfloat16
I32 = mybir.dt.int32
AF = mybir.ActivationFunctionType
ALU = mybir.AluOpType
AX = mybir.AxisListType

ENGS = ("pe", "act", "dve", "pool", "sp")
D = 1024
T = 2048
NS = 16
TT = T + NS
NPG = 2560
PI = math.pi


class Buf:
    __slots__ = ("name", "w", "r", "excl")

    def __init__(self, name="", excl=False):
        self.name = name
        self.w = None
        self.r = {}
        self.excl = excl


class Sched:
    NDMA = 48

    def __init__(self, nc):
        self.nc = nc
        self.ops = {e: [] for e in ENGS}
        self.cnt = {e: 0 for e in ENGS}
        self.seen = {e: {} for e in ENGS}
        self.dma_cnt = [0] * self.NDMA
        self.dma_next = 0
        self.sems = {}
        self.sw_gen = {}

    def _need(self, eng, ev, waits):
        if ev is None:
            return
        key, val = ev
        if key == eng and eng == "pe":
            return
        if self.seen[eng].get(key, 0) >= val:
            return
        waits[key] = max(waits.get(key, 0), val)

    def _deps(self, eng, reads, writes, waits=None):
        waits = {} if waits is None else waits
        for b in reads:
            self._need(eng, b.w, waits)
        for b in writes:
            self._need(eng, b.w, waits)
            for k, v in b.r.items():
                self._need(eng, (k, v), waits)
        for k, v in waits.items():
            self.seen[eng][k] = v
            self.ops[eng].append(("wait", k, v))

    def _mark(self, ev, reads, writes):
        k, v = ev
        for b in reads:
            if b.r.get(k, 0) < v:
                b.r[k] = v
        for b in writes:
            b.w = ev
            b.r = {}

    @staticmethod
    def _split(reads, writes):
        ex = [b for b in reads if b.excl]
        if ex:
            reads = [b for b in reads if not b.excl]
            writes = list(writes) + ex
        return reads, writes

    def op(self, eng, fn, reads=(), writes=()):
        reads, writes = self._split(reads, writes)
        self._deps(eng, reads, writes)
        self.cnt[eng] += 1
        ev = (eng, self.cnt[eng])
        self.ops[eng].append(("op", fn, eng, 1))
        self._mark(ev, reads, writes)
        return ev

    def pe(self, fns, reads=(), writes=()):
        reads, writes = self._split(reads, writes)
        self._deps("pe", reads, writes)
        for fn in fns[:-1]:
            self.ops["pe"].append(("op", fn, None, 0))
        self.cnt["pe"] += 1
        ev = ("pe", self.cnt["pe"])
        self.ops["pe"].append(("op", fns[-1], "pe", 1))
        self._mark(ev, reads, writes)
        return ev

    def dma(self, q, fn, reads=(), writes=()):
        s = self.dma_next
        self.dma_next = (self.dma_next + 1) % self.NDMA
        key = ("d", s)
        waits = {}
        if self.dma_cnt[s]:
            self._need(q, (key, 16 * self.dma_cnt[s]), waits)
        self._deps(q, reads, writes, waits)
        self.dma_cnt[s] += 1
        ev = (key, 16 * self.dma_cnt[s])
        self.ops[q].append(("op", fn, key, 16))
        self._mark(ev, reads, writes)
        return ev

    def barrier(self):
        tgt = {e: self.cnt[e] for e in ENGS if self.cnt[e]}
        for i in range(self.NDMA):
            if self.dma_cnt[i]:
                tgt[("d", i)] = 16 * self.dma_cnt[i]
        for slot, gen in self.sw_gen.items():
            tgt[("k", slot, gen)] = 16
        for e in ENGS:
            for k, v in tgt.items():
                if k == e:
                    continue
                if self.seen[e].get(k, 0) >= v:
                    continue
                self.seen[e][k] = v
                self.ops[e].append(("wait", k, v))

    def swdma(self, slot, fn, reads=(), writes=()):
        q = "pool"
        gen = self.sw_gen.get(slot, 0) + 1
        self.sw_gen[slot] = gen
        key = ("k", slot, gen)
        reads, writes = self._split(reads, writes)
        waits = {}
        if gen > 1:
            self._need(q, (("k", slot, gen - 1), 16), waits)
        self._deps(q, reads, writes, waits)
        self.ops[q].append(("clear", key))
        ev = (key, 16)
        self.ops[q].append(("op", fn, key, 16))
        self._mark(ev, reads, writes)
        return ev

    def wait_all(self, eng, bufs):
        waits = {}
        for b in bufs:
            self._need(eng, b.w, waits)
        for k, v in waits.items():
            self.seen[eng][k] = v
            self.ops[eng].append(("wait", k, v))

    def emit(self, stack):
        nc = self.nc
        keys = list(ENGS) + [("d", i) for i in range(self.NDMA)]
        for k in keys:
            nm = k if isinstance(k, str) else "d%d" % k[1]
            self.sems[k] = stack.enter_context(nc.semaphore("s_" + nm))
        for slot in self.sw_gen:
            self.sems[("ks", slot)] = stack.enter_context(nc.semaphore("s_k%d" % slot))
        block = stack.enter_context(nc.Block())
        base = self.sems

        class _S(dict):
            def __missing__(self_, k):
                if isinstance(k, tuple) and k[0] == "k":
                    return base[("ks", k[1])]
                return base[k]
        sems = _S()

        def run(name):
            def body(e):
                for item in self.ops[name]:
                    if item[0] == "wait":
                        e.wait_ge(sems[item[1]], item[2])
                    elif item[0] == "clear":
                        e.sem_clear(sems[item[1]])
                    else:
                        _, fn, key, inc = item
                        ins = fn(e)
                        if inc:
                            ins.then_inc(sems[key], inc)
            return body

        block.tensor(run("pe"))
        block.scalar(run("act"))
        block.vector(run("dve"))
        block.gpsimd(run("pool"))
        block.sync(run("sp"))


WNAMES = [
    ("g_pre_a", (2, 1024)), ("g_post_a", (2, 1024)), ("w_mod_a", (2, 1024, 3072)), ("b_mod_a", (2, 3072)),
    ("w_in_a", (2, 1024, 2048)), ("lam_re", (2, 64, 64)), ("lam_im", (2, 64, 64)), ("log_dt", (2, 64)),
    ("bmat_re", (2, 64, 64, 16)), ("bmat_im", (2, 64, 64, 16)), ("cmat_re", (2, 64, 16, 64)),
    ("cmat_im", (2, 64, 16, 64)), ("d_skip", (2, 1024)), ("w_glu", (2, 1024, 1024)), ("b_glu", (2, 1024)),
    ("w_out_a", (2, 1024, 1024)), ("g_kv", (1, 1024)), ("w_mod_kv", (1, 1024, 2048)), ("b_mod_kv", (1, 2048)),
    ("w_kv", (1, 1024, 2048)), ("g_pre_b", (2, 1024)), ("g_post_b", (2, 1024)), ("w_mod_b", (2, 1024, 3072)),
    ("b_mod_b", (2, 3072)), ("w_in_b", (2, 1024, 2048)), ("sb_bias", (2, 16)), ("w_out_b", (2, 1024, 1024)),
]


def make_consts():
    c = {}
    c["identf"] = np.eye(128, dtype=np.float32)
    c["identb"] = np.eye(128, dtype=np.float32).astype(ml_dtypes.bfloat16)
    k = np.arange(128)
    c["ntincl"] = (-(k[:, None] >= k[None, :]).astype(np.float32)).astype(ml_dtypes.bfloat16)
    c["ntrest"] = (-(k[:, None] < k[None, :]).astype(np.float32)).astype(ml_dtypes.bfloat16)
    c["negmask"] = np.where(k[:, None] >= k[None, :], -30000.0, 0.0).astype(np.float32).astype(ml_dtypes.bfloat16)
    c["tvec"] = np.broadcast_to(np.arange(512, dtype=np.float32), (128, 512)).copy()
    c["iota"] = np.arange(128, dtype=np.float32).reshape(128, 1)
    sel = np.zeros((128, 2, 64), np.float32)
    for gl in range(8):
        for cc in range(16):
            sel[gl * 16 + cc, gl % 2, (gl // 2) * 16 + cc] = 1.0
    c["sel"] = sel
    selm = np.zeros((128, 2), np.float32)
    selm[:64, 0] = 1.0
    selm[64:, 1] = 1.0
    c["selm"] = selm
    pm = np.zeros((128, 4), np.float32)
    for pl in range(4):
        pm[32 * pl:32 * pl + 32, pl] = 1.0
    c["pairm"] = pm
    c["ntinclf"] = (-(k[:, None] >= k[None, :]).astype(np.float32))
    c["nonesf"] = -np.ones((128, 128), np.float32)
    sm = np.ones((128, 16, 16), np.float32)
    sm[:, :, 0] = 0.0
    c["scanm"] = sm.reshape(128, 256)
    c["onesb"] = np.ones((128, 1), np.float32).astype(ml_dtypes.bfloat16)
    c["pm8"] = (np.arange(128) % 8).astype(np.float32).reshape(128, 1)
    c["tstrf"] = (k[:, None] > k[None, :]).astype(np.float32)
    return c


CONST_SPECS = None


def build(stage=99, npg=NPG, sub=99, shard=False, nvc=8):
    nc = bass.Bass("TRN2", target_bir_lowering=False)
    S = Sched(nc)

    def din(name, shape, dt=F32):
        return nc.dram_tensor(name, list(shape), dt, kind="ExternalInput").ap()

    def dout(name, shape):
        return nc.dram_tensor(name, list(shape), F32, kind="ExternalOutput").ap()

    def dscr(name, shape, dt=F32):
        return nc.dram_tensor(name, list(shape), dt, kind="Internal").ap()

    xp_a = din("xp", (nvc, T, D)); xs_a = din("xs", (nvc, NS, D)); cc_a = din("cc", (nvc, NS + 1, D))
    st_re_a = din("st_re", (nvc, 2, NS, 4096)); st_im_a = din("st_im", (nvc, 2, NS, 4096))
    pt_a = din("pt", (nvc, 1, 256), I32)
    ck = din("ck", (npg * 128, D)); cv = din("cv", (npg * 128, D))
    Wd = {n: din(n, s) for n, s in WNAMES}
    consts = make_consts()
    Cd = {}
    for n, a in consts.items():
        Cd[n] = din("c_" + n, a.shape, BF16 if a.dtype == ml_dtypes.bfloat16 else F32)

    yp_a = dout("yp", (nvc, T, D)); ys_a = dout("ys", (nvc, NS, D))
    pssm_re_a = dout("pssm_re", (nvc, 2, 4096)); pssm_im_a = dout("pssm_im", (nvc, 2, 4096))
    pk_a = dout("pk", (nvc, T, D)); pv_a = dout("pv", (nvc, T, D))
    sssm_re_a = dout("sssm_re", (nvc, 2, NS, 4096)); sssm_im_a = dout("sssm_im", (nvc, 2, NS, 4096))
    sk_a = dout("sk", (nvc, NS, D)); sv_a = dout("sv", (nvc, NS, D))
    out_bufs = []

    def obuf():
        b = Buf()
        out_bufs.append(b)
        return b

    xres = dscr("xres", (T, D)); xsres = dscr("xsres", (NS, D))
    modtok = dscr("modtok", (5, NS + 1, 3072))
    qscr = dscr("qscr", (NS, D))
    zscr = dscr("zscr", (8, 128, TT), BF16)
    g2scr = dscr("g2scr", (8, 128, TT), BF16)
    b_xres = [Buf("xres%d" % i) for i in range(17)]
    b_modtok = [Buf() for _ in range(5)]
    b_qscr = Buf(); b_zscr = [Buf() for _ in range(8)]; b_g2scr = [Buf() for _ in range(8)]

    DV = lambda fn, r=(), w=(): S.op("dve", fn, r, w)
    AC = lambda fn, r=(), w=(): S.op("act", fn, r, w)
    PO = lambda fn, r=(), w=(): S.op("pool", fn, r, w)
    PEg = lambda fns, r=(), w=(): S.pe(fns, r, w)
    DM = lambda fn, r=(), w=(), q="sp": S.dma(q, fn, r, w)

    st = ExitStack()
    with st:
        def sb(name, shape, dt=F32):
            return st.enter_context(nc.sbuf_tensor(name, list(shape), dt))

        CT = {}
        b_const = Buf("const")
        for n, a in consts.items():
            t = sb("k_" + n, a.shape, BF16 if a.dtype == ml_dtypes.bfloat16 else F32)
            CT[n] = t
            DM((lambda t=t, n=n: (lambda e: e.dma_start(out=t[:], in_=Cd[n])))(), w=[b_const])
        identf, identb = CT["identf"], CT["identb"]
        cm_pi = sb("cm_pi", (128, 1)); c_one = sb("c_one", (128, 1)); c_eps = sb("c_eps", (128, 1)); ones512 = sb("ones512", (128, 512))
        PO(lambda e: e.memset(cm_pi[:], (PI / 2) * (1 - 1e-6)), w=[b_const])
        PO(lambda e: e.memset(c_one[:], 1.0), w=[b_const])
        PO(lambda e: e.memset(c_eps[:], 1e-6), w=[b_const])
        PO(lambda e: e.memset(ones512[:], 1.0), w=[b_const])

        PSt = [st.enter_context(nc.psum_tensor("psb%d" % i, [128, 512], F32)) for i in range(8)]
        PS = [t[:] for t in PSt]
        b_ps = [Buf("ps%d" % i, excl=True) for i in range(8)]
        psctr = {}

        def next_ps(lo, hi):
            c = psctr.get((lo, hi), 0)
            psctr[(lo, hi)] = c + 1
            return lo + c % (hi - lo)

        modF = sb("modF", (128, 5, 24, NS + 1)); b_modF = [Buf("modF%d" % i) for i in range(5)]
        hT = sb("hT", (128, 8, TT), BF16); b_hT = Buf("hT")
        big1 = sb("big1", (128, 8, TT), BF16); b_big1 = [Buf("big1_%d" % i) for i in range(8)]
        wst = [sb("wst%d" % i, (128, 8, 128)) for i in range(2)]; b_wst = [Buf(), Buf()]
        wbf = [sb("wbf%d" % i, (128, 8, 128), BF16) for i in range(2)]; b_wbf = [Buf(), Buf()]
        xt = [sb("xt%d" % i, (128, 1024)) for i in range(2)]; b_xt = [Buf(), Buf()]
        xn = [sb("xn%d" % i, (128, 1024), BF16) for i in range(2)]; b_xn = [Buf(), Buf()]
        ssq = sb("ssq", (128, 4)); b_ssq = Buf()
        gcol = sb("gcol", (128, 8)); b_gcol = Buf()
        Acol = sb("Acol", (128, 8)); Asm = sb("Asm", (128, 8, NS)); b_A = Buf()
        tsm = sb("tsm", (128, 8, NS)); b_tsm = Buf()
        GG = sb("GG", (128, 1024)); GGs = sb("GGs", (NS, 1024)); b_GG = Buf()
        y2 = sb("y2", (128, 1024)); b_y2 = Buf()
        g2l = [sb("g2l%d" % i, (128, 8, 128), BF16) for i in range(2)]; b_g2l = [Buf(), Buf()]
        idx = sb("idx", (128, 256), I32); b_idx = Buf()
        biasb = sb("biasb", (128, 16)); b_biasb = Buf()
        ARENA_W = 19000
        arena = sb("arena", (128, ARENA_W))
        atop = [0]

        def aalloc(shape, dt=F32):
            n = 1
            for s_ in shape[1:]:
                n *= s_
            words = n if dt != BF16 else (n + 1) // 2
            off = atop[0]
            atop[0] += words
            assert atop[0] <= ARENA_W, ("arena overflow", atop[0])
            v = arena[0:shape[0], off:off + words]
            if dt == BF16:
                v = v.bitcast(BF16)
            elif dt == I32:
                v = v.bitcast(I32)
            if len(shape) == 3:
                v = v.rearrange("p (a b) -> p a b", a=shape[1])
            elif len(shape) == 4:
                v = v.rearrange("p (a b c) -> p a b c", a=shape[1], b=shape[2])
            return v

        def arelease(mark):
            S.barrier()
            atop[0] = mark

        wctr = [0]

        def load_w(wap, c0):
            i = wctr[0] % 2
            wctr[0] += 1
            src = wap[:, c0:c0 + 128].rearrange("(k p) n -> p k n", p=128)
            DM(lambda e: e.dma_start(out=wst[i][:], in_=src), w=[b_wst[i]])
            PO(lambda e: e.tensor_copy(out=wbf[i][:], in_=wst[i][:]), r=[b_wst[i]], w=[b_wbf[i]])
            return wbf[i], b_wbf[i]

        pieces = [(0, 512), (512, 512), (1024, 512), (1536, 512), (2048, NS)]

        MAGIC = 12582912.0
        SSC = -2 * PI * (1 - 1e-6)

        def sincos_turns(y, cos_out, sin_out, rb, wb):
            DV(lambda e: e.tensor_scalar(out=cos_out, in0=y, scalar1=0.25, scalar2=MAGIC, op0=ALU.add, op1=ALU.add), r=rb, w=wb)
            DV(lambda e: e.scalar_tensor_tensor(out=cos_out, in0=cos_out, scalar=MAGIC, in1=y, op0=ALU.subtract, op1=ALU.subtract), r=rb + wb, w=wb)
            AC(lambda e: e.activation(out=cos_out, in_=cos_out, func=AF.Sin, scale=SSC, bias=cm_pi[:, 0:1]), r=wb + [b_const], w=wb)
            DV(lambda e: e.tensor_scalar(out=sin_out, in0=y, scalar1=MAGIC, scalar2=MAGIC, op0=ALU.add, op1=ALU.subtract), r=rb, w=wb)
            DV(lambda e: e.tensor_tensor(out=sin_out, in0=sin_out, in1=y, op=ALU.subtract), r=rb + wb, w=wb)
            AC(lambda e: e.activation(out=sin_out, in_=sin_out, func=AF.Sin, scale=SSC), r=wb, w=wb)

        def rstd_from(ap_):
            np_ = ap_.shape[0]
            AC(lambda e: e.activation(out=ap_, in_=ap_, func=AF.Sqrt, scale=1.0 / D, bias=c_eps[0:np_, 0:1]), r=[b_ssq, b_const], w=[b_ssq])
            DV(lambda e: e.reciprocal(out=ap_, in_=ap_), r=[b_ssq], w=[b_ssq])

        def prenorm(gname, li, mi):
            DM(lambda e: e.dma_start(out=gcol[:], in_=Wd[gname][li:li + 1, :].rearrange("o (a p) -> p (o a)", p=128)), w=[b_gcol])
            DV(lambda e: e.scalar_tensor_tensor(out=Acol[:], in0=modF[:, mi, 8:16, 0], scalar=1.0, in1=gcol[:],
                                                op0=ALU.add, op1=ALU.mult), r=[b_modF[mi], b_gcol], w=[b_A])
            DV(lambda e: e.scalar_tensor_tensor(out=Asm[:], in0=modF[:, mi, 8:16, 1:NS + 1], scalar=1.0,
                                                in1=gcol[:].unsqueeze(2).to_broadcast([128, 8, NS]),
                                                op0=ALU.add, op1=ALU.mult), r=[b_modF[mi], b_gcol], w=[b_A])
            for i in range(17):
                j = i % 2
                np_ = 128 if i < 16 else NS
                src = xres[128 * i:128 * i + 128, :] if i < 16 else xsres
                DM(lambda e, j=j, np_=np_, src=src: e.dma_start(out=xt[j][0:np_, :], in_=src), r=[b_xres[i]], w=[b_xt[j]])
                AC(lambda e, j=j, np_=np_: e.activation(out=xn[j][0:np_, :], in_=xt[j][0:np_, :], func=AF.Square,
                                                        accum_out=ssq[0:np_, 0:1]), r=[b_xt[j]], w=[b_xn[j], b_ssq])
                rstd_from(ssq[0:np_, 0:1])
                DV(lambda e, j=j, np_=np_: e.tensor_scalar(out=xn[j][0:np_, :], in0=xt[j][0:np_, :], scalar1=ssq[0:np_, 0:1],
                                                           scalar2=None, op0=ALU.mult), r=[b_xt[j], b_ssq], w=[b_xn[j]])
                pi = next_ps(4, 8)
                pst = PS[pi].bitcast(BF16)
                PEg([(lambda e, k=k, j=j, np_=np_, pst=pst: e.transpose(
                    out=pst[:, 128 * k:128 * k + np_], in_=xn[j][0:np_, 128 * k:128 * k + 128], identity=identb[0:np_, 0:np_]))
                    for k in range(8)], r=[b_xn[j], b_const], w=[b_ps[pi]])
                if i < 16:
                    for k in range(8):
                        if k % 2 == 0:
                            AC(lambda e, k=k, i=i, pst=pst: e.activation(
                                out=hT[:, k, 128 * i:128 * i + 128], in_=pst[:, 128 * k:128 * k + 128], func=AF.Identity,
                                scale=Acol[:, k:k + 1], bias=modF[:, mi, k, 0:1]), r=[b_ps[pi], b_A, b_modF[mi]], w=[b_hT])
                        else:
                            DV(lambda e, k=k, i=i, pst=pst: e.tensor_scalar(
                                out=hT[:, k, 128 * i:128 * i + 128], in0=pst[:, 128 * k:128 * k + 128],
                                scalar1=Acol[:, k:k + 1], scalar2=modF[:, mi, k, 0:1], op0=ALU.mult, op1=ALU.add),
                                r=[b_ps[pi], b_A, b_modF[mi]], w=[b_hT])
                else:
                    DV(lambda e, pst=pst: e.tensor_tensor(
                        out=tsm[:], in0=pst.rearrange("p (k t) -> p k t", t=128)[:, :, 0:NS], in1=Asm[:], op=ALU.mult),
                        r=[b_ps[pi], b_A], w=[b_tsm])
                    DV(lambda e: e.tensor_tensor(out=hT[:, :, T:TT], in0=tsm[:], in1=modF[:, mi, 0:8, 1:NS + 1], op=ALU.add),
                       r=[b_tsm, b_modF[mi]], w=[b_hT])

        def outproj_postnorm(wname, li, gname, mi, final=False):
            mk = atop[0]
            wout_bf = aalloc((128, 8, 1024), BF16); b_wout = Buf()
            DM(lambda e: e.dma_start(out=y2[:], in_=Wd[gname][li:li + 1, :].partition_broadcast(128)), w=[b_y2])
            DM(lambda e: e.dma_start(out=GG[:], in_=modtok[mi, 0:1, 2048:3072].partition_broadcast(128)), r=[b_modtok[mi]], w=[b_GG])
            DM(lambda e: e.dma_start(out=GGs[:], in_=modtok[mi, 1:NS + 1, 2048:3072]), r=[b_modtok[mi]], w=[b_GG])
            DV(lambda e: e.tensor_tensor(out=GG[:], in0=GG[:], in1=y2[:], op=ALU.mult), r=[b_GG, b_y2], w=[b_GG])
            DV(lambda e: e.tensor_tensor(out=GGs[:], in0=GGs[:], in1=y2[0:NS, :], op=ALU.mult), r=[b_GG, b_y2], w=[b_GG])
            for g in range(8):
                wt, wb = load_w(Wd[wname][li], 128 * g)
                PO(lambda e, g=g, wt=wt: e.tensor_copy(out=wout_bf[:, :, 128 * g:128 * g + 128], in_=wt[:]), r=[wb], w=[b_wout])
            for i in range(17):
                j = i % 2
                np_ = 128 if i < 16 else NS
                c0 = 128 * i if i < 16 else T
                src = xres[128 * i:128 * i + 128, :] if i < 16 else xsres
                DM(lambda e, j=j, np_=np_, src=src: e.dma_start(out=xt[j][0:np_, :], in_=src), r=[b_xres[i]], w=[b_xt[j]])
                DM(lambda e, j=j, np_=np_, c0=c0: e.dma_start(out=g2l[j][:, :, 0:np_], in_=g2scr[:, :, c0:c0 + np_].rearrange("m p t -> p m t")),
                   r=b_g2scr, w=[b_g2l[j]])
                pis = [next_ps(4, 8), next_ps(4, 8)]
                for h in range(2):
                    PEg([(lambda e, k=k, h=h, np_=np_, j=j, pi=pis[h]: e.matmul(
                        PS[pi][0:np_, :], lhsT=g2l[j][:, k, 0:np_], rhs=wout_bf[:, k, 512 * h:512 * h + 512],
                        start=(k == 0), stop=(k == 7))) for k in range(8)], r=[b_g2l[j], b_wout], w=[b_ps[pis[h]]])
                    AC(lambda e, h=h, np_=np_, j=j, pi=pis[h]: e.activation(
                        out=xn[j][0:np_, 0:512], in_=PS[pi][0:np_, :], func=AF.Square, accum_out=ssq[0:np_, 1 + h:2 + h]),
                        r=[b_ps[pis[h]]], w=[b_xn[j], b_ssq])
                DV(lambda e, np_=np_: e.tensor_tensor(out=ssq[0:np_, 3:4], in0=ssq[0:np_, 1:2], in1=ssq[0:np_, 2:3], op=ALU.add),
                   r=[b_ssq], w=[b_ssq])
                rstd_from(ssq[0:np_, 3:4])
                gg = GG if i < 16 else GGs
                for h in range(2):
                    DV(lambda e, h=h, np_=np_, gg=gg, pi=pis[h]: e.scalar_tensor_tensor(
                        out=y2[0:np_, 512 * h:512 * h + 512], in0=PS[pi][0:np_, :], scalar=ssq[0:np_, 3:4],
                        in1=gg[0:np_, 512 * h:512 * h + 512], op0=ALU.mult, op1=ALU.mult),
                        r=[b_ps[pis[h]], b_ssq, b_GG], w=[b_y2])
                PO(lambda e, np_=np_, j=j: e.tensor_tensor(out=y2[0:np_, :], in0=y2[0:np_, :], in1=xt[j][0:np_, :], op=ALU.add),
                   r=[b_y2, b_xt[j]], w=[b_y2])
                if final:
                    dst = yp[128 * i:128 * i + 128, :] if i < 16 else ys
                    DM(lambda e, np_=np_, dst=dst: e.dma_start(out=dst, in_=y2[0:np_, :]), r=[b_y2], w=[obuf()])
                else:
                    DM(lambda e, np_=np_, src=src: e.dma_start(out=src, in_=y2[0:np_, :]), r=[b_y2], w=[b_xres[i]])
            arelease(mk)

        def s5_layer(li):
            mi = li
            mk_layer = atop[0]
            Rt = [aalloc((128, 32)) for _ in range(16)]
            lamr, lami, ldt, dtR, rho, th, car, cai, wre, wim, tmpa, tmpb, tmpc, thn, em1, s2h = Rt
            b_R = Buf("R")
            TB = aalloc((128, 8, 2, 128), BF16); b_TB = Buf()
            ZZ = aalloc((128, 2, 32, 32), BF16); b_ZZ = Buf()
            h0R = aalloc((128, 2, 32, NS)); b_h0R = Buf()
            hsn = aalloc((128, 2, 32, NS)); b_hsn = Buf()
            fin = aalloc((128, 32, 2)); b_fin = Buf()
            dcol = aalloc((128, 8)); bgcol = aalloc((128, 8)); b_dcol = Buf()
            mk_setup = atop[0]
            BR = aalloc((128, 32, 16)); BI = aalloc((128, 32, 16)); b_B = Buf()
            XRr = aalloc((128, 32, 16)); XRi = aalloc((128, 32, 16)); XRt = aalloc((128, 32, 16))
            XZ = [aalloc((128, 32, 2, 16)) for i in range(2)]; b_XZ = Buf()
            Cnat = aalloc((128, 8, 2, 64)); b_Cnat = Buf()
            CR = aalloc((128, 32, 2, 16)); b_CR = Buf()
            h0 = aalloc((NS, 1024)); b_h0 = Buf()
            R = [b_R]
            DM(lambda e: e.dma_start(out=lamr, in_=Wd["lam_re"][li].rearrange("(q s) p -> (s p) q", s=2)), w=R)
            DM(lambda e: e.dma_start(out=lami, in_=Wd["lam_im"][li].rearrange("(q s) p -> (s p) q", s=2)), w=R)
            for s_ in range(2):
                DM(lambda e, s_=s_: e.dma_start(
                    out=ldt[64 * s_:64 * s_ + 64, :],
                    in_=Wd["log_dt"][li:li + 1, :].rearrange("o (q s) -> o q s", s=2)[:, :, s_].partition_broadcast(64)), w=R)
            AC(lambda e: e.activation(out=dtR, in_=ldt, func=AF.Exp), r=R, w=R)
            DV(lambda e: e.tensor_tensor(out=tmpa, in0=lamr, in1=dtR, op=ALU.mult), r=R, w=R)
            DV(lambda e: e.tensor_scalar(out=em1, in0=tmpa, scalar1=0.2, scalar2=1.0, op0=ALU.mult, op1=ALU.add), r=R, w=R)
            for dv_ in (0.25, 1.0 / 3.0, 0.5):
                DV(lambda e: e.tensor_tensor(out=em1, in0=em1, in1=tmpa, op=ALU.mult), r=R, w=R)
                DV(lambda e, dv_=dv_: e.tensor_scalar(out=em1, in0=em1, scalar1=dv_, scalar2=1.0, op0=ALU.mult, op1=ALU.add), r=R, w=R)
            DV(lambda e: e.tensor_tensor(out=em1, in0=em1, in1=tmpa, op=ALU.mult), r=R, w=R)
            DV(lambda e: e.tensor_scalar(out=rho, in0=em1, scalar1=1.0, scalar2=None, op0=ALU.add), r=R, w=R)
            DV(lambda e: e.tensor_tensor(out=th, in0=lami, in1=dtR, op=ALU.mult), r=R, w=R)
            DV(lambda e: e.tensor_scalar(out=thn, in0=th, scalar1=1.0 / (2 * PI), scalar2=None, op0=ALU.mult), r=R, w=R)
            DV(lambda e: e.tensor_scalar(out=tmpc, in0=th, scalar1=0.5 / (2 * PI), scalar2=None, op0=ALU.mult), r=R, w=R)
            sincos_turns(tmpc, tmpb, s2h, R, R)
            sincos_turns(thn, car, cai, R, R)
            DV(lambda e: e.tensor_tensor(out=s2h, in0=s2h, in1=s2h, op=ALU.mult), r=R, w=R)
            DV(lambda e: e.tensor_tensor(out=tmpa, in0=em1, in1=car, op=ALU.mult), r=R, w=R)
            DV(lambda e: e.scalar_tensor_tensor(out=tmpa, in0=s2h, scalar=-2.0, in1=tmpa, op0=ALU.mult, op1=ALU.add), r=R, w=R)
            DV(lambda e: e.tensor_tensor(out=car, in0=car, in1=rho, op=ALU.mult), r=R, w=R)
            DV(lambda e: e.tensor_tensor(out=cai, in0=cai, in1=rho, op=ALU.mult), r=R, w=R)
            DV(lambda e: e.tensor_tensor(out=tmpb, in0=lamr, in1=lamr, op=ALU.mult), r=R, w=R)
            DV(lambda e: e.tensor_tensor(out=tmpc, in0=lami, in1=lami, op=ALU.mult), r=R, w=R)
            DV(lambda e: e.tensor_tensor(out=tmpb, in0=tmpb, in1=tmpc, op=ALU.add), r=R, w=R)
            DV(lambda e: e.reciprocal(out=tmpb, in_=tmpb), r=R, w=R)
            DV(lambda e: e.tensor_tensor(out=wre, in0=tmpa, in1=lamr, op=ALU.mult), r=R, w=R)
            DV(lambda e: e.tensor_tensor(out=tmpc, in0=cai, in1=lami, op=ALU.mult), r=R, w=R)
            DV(lambda e: e.tensor_tensor(out=wre, in0=wre, in1=tmpc, op=ALU.add), r=R, w=R)
            DV(lambda e: e.tensor_tensor(out=wre, in0=wre, in1=tmpb, op=ALU.mult), r=R, w=R)
            DV(lambda e: e.tensor_tensor(out=wim, in0=cai, in1=lamr, op=ALU.mult), r=R, w=R)
            DV(lambda e: e.tensor_tensor(out=tmpc, in0=tmpa, in1=lami, op=ALU.mult), r=R, w=R)
            DV(lambda e: e.tensor_tensor(out=wim, in0=wim, in1=tmpc, op=ALU.subtract), r=R, w=R)
            DV(lambda e: e.tensor_tensor(out=wim, in0=wim, in1=tmpb, op=ALU.mult), r=R, w=R)
            if sub == 1:
                return
            DM(lambda e: e.dma_start(out=BR, in_=Wd["bmat_re"][li].rearrange("(q s) p c -> (s p) q c", s=2)), w=[b_B])
            DM(lambda e: e.dma_start(out=BI, in_=Wd["bmat_im"][li].rearrange("(q s) p c -> (s p) q c", s=2)), w=[b_B])
            wreb = wre.unsqueeze(2).to_broadcast([128, 32, 16])
            wimb = wim.unsqueeze(2).to_broadcast([128, 32, 16])
            RB = [b_R, b_B]
            DV(lambda e: e.tensor_tensor(out=XRr, in0=BR, in1=wreb, op=ALU.mult), r=RB, w=[b_B])
            DV(lambda e: e.tensor_tensor(out=XRt, in0=BI, in1=wimb, op=ALU.mult), r=RB, w=[b_B])
            DV(lambda e: e.tensor_tensor(out=XRr, in0=XRr, in1=XRt, op=ALU.subtract), r=RB, w=[b_B])
            DV(lambda e: e.tensor_tensor(out=XRi, in0=BI, in1=wreb, op=ALU.mult), r=RB, w=[b_B])
            DV(lambda e: e.tensor_tensor(out=XRt, in0=BR, in1=wimb, op=ALU.mult), r=RB, w=[b_B])
            DV(lambda e: e.tensor_tensor(out=XRi, in0=XRi, in1=XRt, op=ALU.add), r=RB, w=[b_B])
            for ri, XR in enumerate((XRr, XRi)):
                for gs in range(2):
                    DV(lambda e, ri=ri, XR=XR, gs=gs: e.tensor_scalar(
                        out=XZ[ri][:, :, gs, :], in0=XR, scalar1=CT["selm"][:, gs:gs + 1], scalar2=None, op0=ALU.mult),
                        r=[b_B, b_const], w=[b_XZ])
            for ct in range(8):
                pi = next_ps(4, 8)
                PEg([(lambda e, ri=ri, ct=ct, pi=pi: e.matmul(
                    PS[pi][:, 128 * ri:128 * ri + 128],
                    lhsT=XZ[ri][:, 4 * ct:4 * ct + 4, :, :].rearrange("p a b c -> p (a b c)"), rhs=identf[:],
                    start=True, stop=True)) for ri in range(2)], r=[b_XZ, b_const], w=[b_ps[pi]])
                DV(lambda e, ct=ct, pi=pi: e.tensor_copy(out=TB[:, ct, :, :].rearrange("p a b -> p (a b)"), in_=PS[pi][:, 0:256]),
                   r=[b_ps[pi]], w=[b_TB])
            if sub == 2:
                return
            DM(lambda e: e.dma_start(out=Cnat[:, :, 0, :], in_=Wd["cmat_re"][li].rearrange("(ct gl) c p -> (gl c) ct p", gl=8)), w=[b_Cnat])
            DM(lambda e: e.dma_start(out=Cnat[:, :, 1, :], in_=Wd["cmat_im"][li].rearrange("(ct gl) c p -> (gl c) ct p", gl=8)), w=[b_Cnat])
            for ct in range(8):
                pi = next_ps(4, 8)
                fns = []
                for ri in range(2):
                    for gs in range(2):
                        fns.append(lambda e, ri=ri, gs=gs, ct=ct, pi=pi: e.matmul(
                            PS[pi][64 * gs:64 * gs + 64, 64 * ri:64 * ri + 64], lhsT=Cnat[:, ct, ri, :], rhs=CT["sel"][:, gs, :],
                            start=True, stop=True))
                PEg(fns, r=[b_Cnat, b_const], w=[b_ps[pi]])
                DV(lambda e, ct=ct, pi=pi: e.tensor_copy(
                    out=CR[:, 4 * ct:4 * ct + 4, :, :].rearrange("p a r c -> p r a c"),
                    in_=PS[pi][:, 0:128].rearrange("p (r a c) -> p r a c", r=2, a=4)), r=[b_ps[pi]], w=[b_CR])
            for ri in range(2):
                for gs in range(2):
                    DV(lambda e, ri=ri, gs=gs: e.tensor_scalar(
                        out=ZZ[:, ri, :, 16 * gs:16 * gs + 16], in0=CR[:, :, ri, :], scalar1=CT["selm"][:, gs:gs + 1],
                        scalar2=(1.0 if ri == 0 else -1.0), op0=ALU.mult, op1=ALU.mult), r=[b_CR, b_const], w=[b_ZZ])
            DM(lambda e: e.dma_start(out=dcol, in_=Wd["d_skip"][li:li + 1, :].rearrange("o (a p) -> p (o a)", p=128)), w=[b_dcol])
            DM(lambda e: e.dma_start(out=bgcol, in_=Wd["b_glu"][li:li + 1, :].rearrange("o (a p) -> p (o a)", p=128)), w=[b_dcol])
            if sub == 3:
                return
            for ri, src in enumerate((st_re, st_im)):
                for q4 in range(4):
                    DM(lambda e, src=src, q4=q4: e.dma_start(out=h0, in_=src[li, :, 1024 * q4:1024 * q4 + 1024]), w=[b_h0])
                    pi = next_ps(4, 8)
                    PEg([(lambda e, qq=qq, pi=pi: e.matmul(
                        PS[pi][:, NS * qq:NS * qq + NS], lhsT=h0[:, 128 * qq:128 * qq + 128], rhs=identf[0:NS, 0:NS],
                        start=True, stop=True)) for qq in range(8)], r=[b_h0, b_const], w=[b_ps[pi]])
                    DV(lambda e, ri=ri, q4=q4, pi=pi: e.tensor_copy(
                        out=h0R[:, ri, 8 * q4:8 * q4 + 8, :].rearrange("p a b -> p (a b)"), in_=PS[pi][:, 0:8 * NS]),
                        r=[b_ps[pi]], w=[b_h0R])
            if sub == 4:
                return
            arelease(mk_setup)
            TBm = aalloc((128, 2, 4, 128), BF16); b_TBm = Buf()
            Zpad = aalloc((128, 2, 4, 128), BF16); b_Zpad = Buf()
            cosT = aalloc((128, 512)); sinT = aalloc((128, 512)); b_cs = Buf()
            rhoT = aalloc((128, 512)); b_rhoT = Buf()
            ytn = aalloc((128, 512)); b_ytn = Buf()
            uT = aalloc((128, TT), BF16); b_uT = Buf()
            zst = aalloc((128, TT), BF16); b_zst = Buf()
            gin = [aalloc((128, 512)) for i in range(2)]; b_gin = Buf()
            gsc = [aalloc((128, 512)) for i in range(2)]; b_gsc = [Buf(), Buf()]
            t1 = aalloc((128, 512)); t2 = aalloc((128, 512)); b_t = Buf()
            t3 = aalloc((128, 512)); t4 = aalloc((128, 512)); b_t34 = Buf()
            Hb = [aalloc((128, 512), BF16) for i in range(2)]; b_Hb = Buf()
            carry = aalloc((128, 2)); b_carry = Buf()
            hsb = aalloc((128, 2, NS), BF16); b_hsb = Buf()
            hso = aalloc((NS, 512)); b_hso = Buf()
            thL = aalloc((128, 4)); b_thL = Buf()

            prenorm("g_pre_a", li, mi)
            if sub == 5:
                for k in range(8):
                    DV(lambda e, k=k: e.tensor_copy(out=t1, in_=hT[:, k, 0:512]), r=[b_hT], w=[b_t])
                    DM(lambda e, k=k: e.dma_start(out=pk[128 * k:128 * k + 128, 0:512], in_=t1), r=[b_t], w=[obuf()])
                DV(lambda e: e.tensor_copy(out=t1[:, 0:8 * NS].rearrange("p (k b) -> p k b", b=NS), in_=hT[:, :, T:TT]), r=[b_hT], w=[b_t])
                DM(lambda e: e.dma_start(out=pv[0:128, 0:8 * NS], in_=t1[:, 0:8 * NS]), r=[b_t], w=[obuf()])
                DM(lambda e: e.dma_start(out=pv[128:256, 0:24 * 17], in_=modF[:, 0, :, :].rearrange("p a r -> p (a r)")), r=[b_modF[0]], w=[obuf()])
                return
            Win = Wd["w_in_a"][li]
            for ct in range(8):
                wu, wub = load_w(Win, 128 * ct)
                for (c0, n) in pieces:
                    pi = next_ps(0, 2)
                    PEg([(lambda e, k=k, c0=c0, n=n, pi=pi, wu=wu: e.matmul(PS[pi][:, 0:n], lhsT=wu[:, k, :], rhs=hT[:, k, c0:c0 + n],
                                                                            start=(k == 0), stop=(k == 7))) for k in range(8)],
                        r=[wub, b_hT], w=[b_ps[pi]])
                    AC(lambda e, c0=c0, n=n, pi=pi: e.activation(out=uT[:, c0:c0 + n], in_=PS[pi][:, 0:n], func=AF.Identity),
                       r=[b_ps[pi]], w=[b_uT])
                wz, wzb = load_w(Win, 1024 + 128 * ct)
                for (c0, n) in pieces:
                    pi = next_ps(0, 2)
                    PEg([(lambda e, k=k, c0=c0, n=n, pi=pi, wz=wz: e.matmul(PS[pi][:, 0:n], lhsT=wz[:, k, :], rhs=hT[:, k, c0:c0 + n],
                                                                            start=(k == 0), stop=(k == 7))) for k in range(8)],
                        r=[wzb, b_hT], w=[b_ps[pi]])
                    AC(lambda e, c0=c0, n=n, pi=pi: e.activation(out=zst[:, c0:c0 + n], in_=PS[pi][:, 0:n], func=AF.Silu),
                       r=[b_ps[pi]], w=[b_zst])
                DM(lambda e, ct=ct: e.dma_start(out=zscr[ct], in_=zst), r=[b_zst], w=[b_zscr[ct]])
                if sub == 6:
                    return
                for ri in range(2):
                    for pl in range(4):
                        PO(lambda e, ri=ri, pl=pl, ct=ct: e.tensor_scalar(
                            out=TBm[:, ri, pl, :], in0=TB[:, ct, ri, :], scalar1=CT["pairm"][:, pl:pl + 1], scalar2=None, op0=ALU.mult),
                            r=[b_TB, b_const], w=[b_TBm])
                PO(lambda e: e.memset(Zpad, 0.0), w=[b_Zpad])
                for pl in range(4):
                    PO(lambda e, pl=pl, ct=ct: e.tensor_copy(out=Zpad[:, :, pl, 32 * pl:32 * pl + 32], in_=ZZ[:, :, 4 * ct + pl, :]),
                       r=[b_ZZ], w=[b_Zpad])
                ypis = [4, 5, 6, 7]
                for pl in range(4):
                    pair = 4 * ct + pl
                    DV(lambda e, pair=pair: e.tensor_scalar(out=rhoT, in0=ones512[:], scalar1=rho[:, pair:pair + 1], scalar2=None,
                                                            op0=ALU.mult), r=[b_R, b_const], w=[b_rhoT])
                    for pc in range(4):
                        c0 = 512 * pc
                        DV(lambda e, pair=pair, c0=c0: e.tensor_scalar(
                            out=ytn, in0=CT["tvec"][:], scalar1=float(c0), scalar2=thn[:, pair:pair + 1], op0=ALU.add, op1=ALU.mult),
                            r=[b_R, b_const], w=[b_ytn])
                        sincos_turns(ytn, cosT, sinT, [b_ytn], [b_cs])
                        pr = 0
                        pim = 1
                        PEg([lambda e, pl=pl, c0=c0, pr=pr: e.matmul(PS[pr], lhsT=TBm[:, 0, pl, :], rhs=uT[:, c0:c0 + 512], start=True, stop=True),
                             lambda e, pl=pl, c0=c0, pim=pim: e.matmul(PS[pim], lhsT=TBm[:, 1, pl, :], rhs=uT[:, c0:c0 + 512], start=True, stop=True)],
                            r=[b_TBm, b_uT], w=[b_ps[pr], b_ps[pim]])
                        rd = [b_ps[pr], b_ps[pim], b_cs]
                        DV(lambda e, pr=pr: e.tensor_tensor(out=t1, in0=PS[pr], in1=cosT, op=ALU.mult), r=rd, w=[b_t])
                        DV(lambda e, pim=pim: e.tensor_tensor(out=t2, in0=PS[pim], in1=sinT, op=ALU.mult), r=rd, w=[b_t])
                        DV(lambda e: e.tensor_tensor(out=gin[0], in0=t1, in1=t2, op=ALU.add), r=[b_t], w=[b_gin])
                        DV(lambda e, pim=pim: e.tensor_tensor(out=t1, in0=PS[pim], in1=cosT, op=ALU.mult), r=rd + [b_gin], w=[b_t])
                        DV(lambda e, pr=pr: e.tensor_tensor(out=t2, in0=PS[pr], in1=sinT, op=ALU.mult), r=rd, w=[b_t])
                        DV(lambda e: e.tensor_tensor(out=gin[1], in0=t1, in1=t2, op=ALU.subtract), r=[b_t], w=[b_gin])
                        for ri in range(2):
                            init = 0.0 if pc == 0 else carry[:, ri:ri + 1]
                            DV(lambda e, ri=ri, init=init: e.tensor_tensor_scan(
                                out=gsc[ri], data0=rhoT, data1=gin[ri], initial=init, op0=ALU.mult, op1=ALU.add),
                                r=[b_gin, b_rhoT, b_carry], w=[b_gsc[ri]])
                        DV(lambda e: e.tensor_copy(out=carry[:, 0:1], in_=gsc[0][:, 511:512]), r=[b_gsc[0]], w=[b_carry])
                        DV(lambda e: e.tensor_copy(out=carry[:, 1:2], in_=gsc[1][:, 511:512]), r=[b_gsc[1]], w=[b_carry])
                        if sub == 7 and ct == 0 and pl == 0:
                            DV(lambda e: e.tensor_copy(out=t3, in_=PS[0]), r=[b_ps[0]], w=[b_t34])
                            DV(lambda e: e.tensor_copy(out=t4, in_=PS[1]), r=[b_ps[1]], w=[b_t34])
                            DM(lambda e, pc=pc: e.dma_start(out=pv[0:128, 512 * (pc % 2):512 * (pc % 2) + 512] if pc < 2 else pv[128:256, 512 * (pc % 2):512 * (pc % 2) + 512], in_=t3), r=[b_t34], w=[obuf()])
                            DM(lambda e, pc=pc: e.dma_start(out=pv[256:384, 512 * (pc % 2):512 * (pc % 2) + 512] if pc < 2 else pv[384:512, 512 * (pc % 2):512 * (pc % 2) + 512], in_=t4), r=[b_t34], w=[obuf()])
                            DM(lambda e, pc=pc: e.dma_start(out=pk[0:128, 2 * pc:2 * pc + 2], in_=carry), r=[b_carry], w=[obuf()])
                            if pc == 0:
                                DM(lambda e: e.dma_start(out=pk[128:256, 0:32], in_=rho), r=[b_R], w=[obuf()])
                                DM(lambda e: e.dma_start(out=pk[256:384, 0:32], in_=thn), r=[b_R], w=[obuf()])
                        rg = [b_gsc[0], b_gsc[1], b_cs]
                        PO(lambda e: e.tensor_tensor(out=t3, in0=gsc[0], in1=cosT, op=ALU.mult), r=rg, w=[b_t34])
                        PO(lambda e: e.tensor_tensor(out=t4, in0=gsc[1], in1=sinT, op=ALU.mult), r=rg, w=[b_t34])
                        PO(lambda e: e.tensor_tensor(out=Hb[0], in0=t3, in1=t4, op=ALU.subtract), r=[b_t34], w=[b_Hb])
                        if pc == 3:
                            PO(lambda e, pair=pair: e.tensor_tensor(out=fin[:, pair, 0:1], in0=t3[:, 511:512], in1=t4[:, 511:512], op=ALU.subtract),
                               r=[b_t34], w=[b_fin])
                        PO(lambda e: e.tensor_tensor(out=t3, in0=gsc[1], in1=cosT, op=ALU.mult), r=rg + [b_Hb], w=[b_t34])
                        PO(lambda e: e.tensor_tensor(out=t4, in0=gsc[0], in1=sinT, op=ALU.mult), r=rg, w=[b_t34])
                        PO(lambda e: e.tensor_tensor(out=Hb[1], in0=t3, in1=t4, op=ALU.add), r=[b_t34], w=[b_Hb])
                        if pc == 3:
                            PO(lambda e, pair=pair: e.tensor_tensor(out=fin[:, pair, 1:2], in0=t3[:, 511:512], in1=t4[:, 511:512], op=ALU.add),
                               r=[b_t34], w=[b_fin])
                        yp_i = ypis[pc]
                        PEg([(lambda e, ri=ri, yp_i=yp_i, pl=pl: e.matmul(
                            PS[yp_i], lhsT=Zpad[:, ri, pl, :], rhs=Hb[ri], start=(pl == 0 and ri == 0), stop=(pl == 3 and ri == 1)))
                            for ri in range(2)], r=[b_Zpad, b_Hb], w=[b_ps[yp_i]])
                    PEg([lambda e, pl=pl: e.matmul(PS[2][:, 0:NS], lhsT=TBm[:, 0, pl, :], rhs=uT[:, T:TT], start=True, stop=True),
                         lambda e, pl=pl: e.matmul(PS[2][:, 32:32 + NS], lhsT=TBm[:, 1, pl, :], rhs=uT[:, T:TT], start=True, stop=True)],
                        r=[b_TBm, b_uT], w=[b_ps[2]])
                    arc = car[:, pair:pair + 1]
                    aic = cai[:, pair:pair + 1]
                    DV(lambda e, pair=pair, arc=arc: e.scalar_tensor_tensor(
                        out=hsn[:, 0, pair, :], in0=h0R[:, 0, pair, :], scalar=arc, in1=PS[2][:, 0:NS], op0=ALU.mult, op1=ALU.add),
                        r=[b_h0R, b_R, b_ps[2]], w=[b_hsn])
                    DV(lambda e, pair=pair, aic=aic: e.tensor_scalar(out=t1[:, 0:NS], in0=h0R[:, 1, pair, :], scalar1=aic, scalar2=None, op0=ALU.mult),
                       r=[b_h0R, b_R], w=[b_t])
                    DV(lambda e, pair=pair: e.tensor_tensor(out=hsn[:, 0, pair, :], in0=hsn[:, 0, pair, :], in1=t1[:, 0:NS], op=ALU.subtract),
                       r=[b_t, b_hsn], w=[b_hsn])
                    DV(lambda e, pair=pair, arc=arc: e.scalar_tensor_tensor(
                        out=hsn[:, 1, pair, :], in0=h0R[:, 1, pair, :], scalar=arc, in1=PS[2][:, 32:32 + NS], op0=ALU.mult, op1=ALU.add),
                        r=[b_h0R, b_R, b_ps[2]], w=[b_hsn])
                    DV(lambda e, pair=pair, aic=aic: e.tensor_scalar(out=t1[:, 0:NS], in0=h0R[:, 0, pair, :], scalar1=aic, scalar2=None, op0=ALU.mult),
                       r=[b_h0R, b_R, b_hsn], w=[b_t])
                    DV(lambda e, pair=pair: e.tensor_tensor(out=hsn[:, 1, pair, :], in0=hsn[:, 1, pair, :], in1=t1[:, 0:NS], op=ALU.add),
                       r=[b_t, b_hsn], w=[b_hsn])
                    DV(lambda e, pair=pair: e.tensor_copy(out=hsb, in_=hsn[:, :, pair, :]), r=[b_hsn], w=[b_hsb])
                    PEg([(lambda e, ri=ri, pl=pl: e.matmul(
                        PS[3][:, 0:NS], lhsT=Zpad[:, ri, pl, :], rhs=hsb[:, ri, :], start=(pl == 0 and ri == 0), stop=(pl == 3 and ri == 1)))
                        for ri in range(2)], r=[b_Zpad, b_hsb], w=[b_ps[3]])
                if sub == 7:
                    DM(lambda e: e.dma_start(out=pk[384:512, 0:64], in_=fin.rearrange("p a b -> p (a b)")), r=[b_fin], w=[obuf()])
                    return
                for pc, (c0, n) in enumerate(pieces):
                    pi = ypis[pc] if pc < 4 else 3
                    DV(lambda e, c0=c0, n=n, pi=pi, ct=ct: e.scalar_tensor_tensor(
                        out=t1[:, 0:n], in0=uT[:, c0:c0 + n], scalar=dcol[:, ct:ct + 1], in1=PS[pi][:, 0:n], op0=ALU.mult, op1=ALU.add),
                        r=[b_uT, b_dcol, b_ps[pi]], w=[b_t])
                    AC(lambda e, c0=c0, n=n, ct=ct: e.activation(out=big1[:, ct, c0:c0 + n], in_=t1[:, 0:n], func=AF.Gelu),
                       r=[b_t], w=[b_big1[ct]])
            if sub == 8:
                return
            for ri, dst in enumerate((pssm_re, pssm_im)):
                DM(lambda e, ri=ri, dst=dst: e.dma_start(
                    out=dst[li:li + 1, :].rearrange("o (q sp) -> sp (o q)", sp=128), in_=fin[:, :, ri]), r=[b_fin], w=[obuf()])
            for ri, dst in enumerate((sssm_re, sssm_im)):
                for q4 in range(8):
                    pi = next_ps(4, 8)
                    PEg([(lambda e, ri=ri, q=4 * q4 + qq, qq=qq, pi=pi: e.matmul(
                        PS[pi][0:NS, 128 * qq:128 * qq + 128], lhsT=hsn[:, ri, q, :], rhs=identf[:], start=True, stop=True))
                        for qq in range(4)], r=[b_hsn, b_const], w=[b_ps[pi]])
                    DV(lambda e, pi=pi: e.tensor_copy(out=hso, in_=PS[pi][0:NS, :]), r=[b_ps[pi]], w=[b_hso])
                    DM(lambda e, dst=dst, q4=q4: e.dma_start(out=dst[li, :, 512 * q4:512 * q4 + 512], in_=hso), r=[b_hso], w=[obuf()])
            if sub == 9:
                return
            Wg = Wd["w_glu"][li]
            for m in range(8):
                wg, wgb = load_w(Wg, 128 * m)
                DM(lambda e, m=m: e.dma_start(out=zst, in_=zscr[m]), r=[b_zscr[m]], w=[b_zst])
                for (c0, n) in pieces:
                    pi = next_ps(0, 4)
                    PEg([(lambda e, k=k, c0=c0, n=n, pi=pi, wg=wg: e.matmul(PS[pi][:, 0:n], lhsT=wg[:, k, :], rhs=big1[:, k, c0:c0 + n],
                                                                            start=(k == 0), stop=(k == 7))) for k in range(8)],
                        r=[wgb] + b_big1, w=[b_ps[pi]])
                    AC(lambda e, n=n, pi=pi, m=m: e.activation(out=t1[:, 0:n], in_=PS[pi][:, 0:n], func=AF.Sigmoid, bias=bgcol[:, m:m + 1]),
                       r=[b_ps[pi], b_dcol], w=[b_t])
                    DV(lambda e, c0=c0, n=n, m=m: e.tensor_tensor(out=t1[:, 0:n], in0=t1[:, 0:n], in1=big1[:, m, c0:c0 + n], op=ALU.mult),
                       r=[b_t, b_big1[m]], w=[b_t])
                    DV(lambda e, c0=c0, n=n: e.tensor_tensor(out=uT[:, c0:c0 + n], in0=t1[:, 0:n], in1=zst[:, c0:c0 + n], op=ALU.mult),
                       r=[b_t, b_zst], w=[b_uT])
                DM(lambda e, m=m: e.dma_start(out=g2scr[m], in_=uT), r=[b_uT], w=[b_g2scr[m]])
            if sub == 10:
                return
            arelease(mk_layer)
            outproj_postnorm("w_out_a", li, "g_post_a", mi)

        def sb_layer(li):
            mi = 3 + li
            mk_l = atop[0]
            qT = aalloc((128, TT), BF16); b_qT = Buf()
            zsm = aalloc((128, TT), BF16); b_zsm = Buf()
            qs_all = aalloc((128, 8, NS)); b_qs = Buf()
            zs_s = aalloc((128, 8, NS), BF16); b_zss = Buf()
            g2st = aalloc((128, TT), BF16); b_g2st = Buf()
            mk_att = atop[0]
            Et = [aalloc((128, 512)) for i in range(3)]; b_E = [Buf() for _ in range(3)]
            SPt = [aalloc((128, 512), BF16) for i in range(3)]; b_SP = [Buf() for _ in range(3)]
            Xt = [aalloc((128, 512)) for i in range(2)]; b_X = [Buf(), Buf()]
            Wt = [aalloc((128, 512), BF16) for i in range(3)]; b_W = [Buf() for _ in range(3)]
            prenorm("g_pre_b", li, mi)
            DM(lambda e: e.dma_start(out=biasb[:], in_=Wd["sb_bias"][li:li + 1, :].partition_broadcast(128)), w=[b_biasb])
            Win = Wd["w_in_b"][li]
            for m in range(8):
                wq, wqb = load_w(Win, 128 * m)
                for (c0, n) in pieces:
                    pi = next_ps(0, 3)
                    PEg([(lambda e, k=k, c0=c0, n=n, pi=pi, wq=wq: e.matmul(PS[pi][:, 0:n], lhsT=wq[:, k, :], rhs=hT[:, k, c0:c0 + n],
                                                                            start=(k == 0), stop=(k == 7))) for k in range(8)],
                        r=[wqb, b_hT], w=[b_ps[pi]])
                    AC(lambda e, c0=c0, n=n, pi=pi: e.activation(out=qT[:, c0:c0 + n], in_=PS[pi][:, 0:n], func=AF.Identity, scale=0.125),
                       r=[b_ps[pi]], w=[b_qT])
                    if c0 == T:
                        DV(lambda e, pi=pi, m=m: e.tensor_scalar(out=qs_all[:, m, :], in0=PS[pi][:, 0:NS], scalar1=0.125, scalar2=None, op0=ALU.mult),
                           r=[b_ps[pi]], w=[b_qs])
                wz, wzb = load_w(Win, 1024 + 128 * m)
                for (c0, n) in pieces:
                    pi = next_ps(0, 3)
                    PEg([(lambda e, k=k, c0=c0, n=n, pi=pi, wz=wz: e.matmul(PS[pi][:, 0:n], lhsT=wz[:, k, :], rhs=hT[:, k, c0:c0 + n],
                                                                            start=(k == 0), stop=(k == 7))) for k in range(8)],
                        r=[wzb, b_hT], w=[b_ps[pi]])
                    AC(lambda e, c0=c0, n=n, pi=pi: e.activation(out=zsm[:, c0:c0 + n], in_=PS[pi][:, 0:n], func=AF.Silu),
                       r=[b_ps[pi]], w=[b_zsm])
                DV(lambda e, m=m: e.tensor_copy(out=zs_s[:, m, :], in_=zsm[:, T:TT]), r=[b_zsm], w=[b_zss])
                for qg in range(4):
                    ob = 6 + qg % 2
                    jmax = 4 * qg + 3
                    tiles = [(j, hh) for j in range(jmax, -1, -1) for hh in range(2)]
                    nt = len(tiles)

                    def geom(j, qg=qg):
                        diag = j >= 4 * qg
                        c_lo = 128 * (j - 4 * qg) if diag else 0
                        return diag, c_lo

                    def stage1(ti, m=m, qg=qg, ob=ob, tiles=tiles, geom=geom):
                        j, hh = tiles[ti]
                        diag, c_lo = geom(j)
                        r0 = 64 * hh
                        sbk = ti % 3
                        eb = ti % 3
                        fns = [lambda e: e.matmul(PS[sbk][:, c_lo:512], lhsT=big1[r0:r0 + 64, m, 128 * j:128 * j + 128],
                                                  rhs=qT[r0:r0 + 64, 512 * qg + c_lo:512 * qg + 512], start=True, stop=(not diag))]
                        if diag:
                            fns.append(lambda e: e.matmul(PS[sbk][:, c_lo:c_lo + 128], lhsT=identb[:], rhs=CT["negmask"][:],
                                                          start=False, stop=True))
                        PEg(fns, r=[b_KT[j], b_qT, b_const], w=[b_ps[sbk]])
                        h = 2 * m + hh
                        AC(lambda e: e.activation(out=Et[eb][:, c_lo:512], in_=PS[sbk][:, c_lo:512], func=AF.Exp, bias=biasb[:, h:h + 1]),
                           r=[b_ps[sbk], b_biasb], w=[b_E[eb]])
                        AC(lambda e: e.activation(out=SPt[eb][:, c_lo:512], in_=Et[eb][:, c_lo:512], func=AF.Ln, bias=c_one[:, 0:1]),
                           r=[b_E[eb], b_const], w=[b_SP[eb]])

                    def stage2(ti, m=m, qg=qg, ob=ob, tiles=tiles, geom=geom):
                        j, hh = tiles[ti]
                        diag, c_lo = geom(j)
                        eb = ti % 3
                        ab = 3 + hh
                        xb = ti % 2
                        fns = []
                        if diag:
                            fns.append(lambda e: e.matmul(PS[ab][:, c_lo:c_lo + 128], lhsT=CT["ntincl"][:], rhs=SPt[eb][:, c_lo:c_lo + 128],
                                                          start=(j == 4 * qg + 3), stop=False, skip_group_check=True))
                            if c_lo + 128 < 512:
                                fns.append(lambda e: e.matmul(PS[ab][:, c_lo + 128:512], lhsT=CT["ntincl"][:], rhs=SPt[eb][:, c_lo + 128:512],
                                                              start=False, stop=False, skip_group_check=True))
                        else:
                            fns.append(lambda e: e.matmul(PS[ab][:, 0:512], lhsT=CT["ntincl"][:], rhs=SPt[eb][:, 0:512],
                                                          start=False, stop=False, skip_group_check=True))
                        PEg(fns, r=[b_SP[eb], b_const], w=[b_ps[ab]])
                        AC(lambda e: e.activation(out=Xt[xb][:, c_lo:512], in_=PS[ab][:, c_lo:512], func=AF.Exp), r=[b_ps[ab]], w=[b_X[xb]])
                        DV(lambda e: e.tensor_tensor(out=Wt[eb][:, c_lo:512], in0=Et[eb][:, c_lo:512], in1=Xt[xb][:, c_lo:512], op=ALU.mult),
                           r=[b_E[eb], b_X[xb]], w=[b_W[eb]])

                    def stage3(ti, m=m, qg=qg, ob=ob, tiles=tiles, geom=geom):
                        j, hh = tiles[ti]
                        diag, c_lo = geom(j)
                        eb = ti % 3
                        ab = 3 + hh
                        r0 = 64 * hh
                        h = 2 * m + hh
                        PEg([lambda e: e.matmul(PS[ab][:, c_lo:512], lhsT=CT["ntrest"][:], rhs=SPt[eb][:, c_lo:512],
                                                start=False, stop=(j == 0), skip_group_check=True)],
                            r=[b_SP[eb], b_const], w=[b_ps[ab]])
                        fns = []
                        if diag:
                            fns.append(lambda e: e.matmul(PS[ob][r0:r0 + 64, c_lo:c_lo + 128], lhsT=big2[:, j, 64 * h:64 * h + 64],
                                                          rhs=Wt[eb][:, c_lo:c_lo + 128], start=(j == 4 * qg + 3), stop=(j == 0), skip_group_check=True))
                            if c_lo + 128 < 512:
                                fns.append(lambda e: e.matmul(PS[ob][r0:r0 + 64, c_lo + 128:512], lhsT=big2[:, j, 64 * h:64 * h + 64],
                                                              rhs=Wt[eb][:, c_lo + 128:512], start=False, stop=(j == 0), skip_group_check=True))
                        else:
                            fns.append(lambda e: e.matmul(PS[ob][r0:r0 + 64, 0:512], lhsT=big2[:, j, 64 * h:64 * h + 64],
                                                          rhs=Wt[eb][:, 0:512], start=False, stop=(j == 0), skip_group_check=True))
                        PEg(fns, r=[b_W[eb], b_V[j]], w=[b_ps[ob]])

                    for s_ in range(nt + 2):
                        if s_ < nt:
                            stage1(s_)
                        if 0 <= s_ - 1 < nt:
                            stage2(s_ - 1)
                        if 0 <= s_ - 2 < nt:
                            stage3(s_ - 2)
                    DV(lambda e, qg=qg, ob=ob: e.tensor_tensor(out=g2st[:, 512 * qg:512 * qg + 512], in0=PS[ob], in1=zsm[:, 512 * qg:512 * qg + 512], op=ALU.mult),
                       r=[b_ps[ob], b_zsm], w=[b_g2st])
                DM(lambda e, m=m: e.dma_start(out=g2scr[m, :, 0:T], in_=g2st[:, 0:T]), r=[b_g2st], w=[b_g2scr[m]])
            arelease(mk_att)
            if shard:
                PO(lambda e: e.memset(g2st[:, 0:8 * NS], 0.0), w=[b_g2st])
                DM(lambda e: e.dma_start(out=g2scr[:, :, T:TT].rearrange("m p b -> p m b"), in_=g2st[:, 0:8 * NS].rearrange("p (m b) -> p m b", b=NS)),
                   r=[b_g2st], w=b_g2scr)
            else:
                hTf = hT[:].bitcast(F32).rearrange("p a b -> p (a b)")
                NKB = 4
                Kt = [hTf[:, 1024 * i:1024 * i + 1024] for i in range(NKB)]; b_Kt = [Buf() for _ in range(NKB)]
                prods = [hTf[:, 1024 * (NKB + i):1024 * (NKB + i) + 1024] for i in range(2)]; b_prods = [Buf(), Buf()]
                wvs = [p_.bitcast(BF16)[:, 0:1024] for p_ in prods]
                Qb = aalloc((128, 1024)); b_Qb = Buf()
                qtok = aalloc((NS, 1024)); b_qtok = Buf()
                zall, zbt, Esb, SPs, Psc, PmS, wall = [aalloc((128, 256)) for _ in range(7)]
                b_z = Buf(); b_sm = Buf(); b_wall = Buf()
                for hf in range(2):
                    PEg([(lambda e, mm=mm, hf=hf: e.matmul(PS[hf][0:NS, 128 * mm:128 * mm + 128], lhsT=qs_all[:, 4 * hf + mm, :], rhs=identf[:],
                                                            start=True, stop=True)) for mm in range(4)], r=[b_qs, b_const], w=[b_ps[hf]])
                    DV(lambda e, hf=hf: e.tensor_copy(out=qtok[:, 512 * hf:512 * hf + 512], in_=PS[hf][0:NS, :]), r=[b_ps[hf]], w=[b_qtok])
                DM(lambda e: e.dma_start(out=qscr, in_=qtok), r=[b_qtok], w=[b_qscr])
                kc = 0
                osb = 7
                for b in range(NS):
                    DM(lambda e, b=b: e.dma_start(out=Qb, in_=qscr[b:b + 1, :].partition_broadcast(128)), r=[b_qscr], w=[b_Qb])
                    for j in range(16):
                        kb = kc % NKB
                        pb_ = kc % 2
                        kc += 1
                        n_ = 16 * b + j
                        DM(lambda e, kb=kb, n_=n_: e.indirect_dma_start(
                            out=Kt[kb], out_offset=None, in_=ck, in_offset=bass.IndirectOffsetOnAxis(ap=idx[:, n_:n_ + 1], axis=0)),
                            r=[b_idx], w=[b_Kt[kb]], q="pool")
                        PO(lambda e, kb=kb, pb_=pb_: e.tensor_tensor(out=prods[pb_], in0=Kt[kb], in1=Qb, op=ALU.mult), r=[b_Kt[kb], b_Qb], w=[b_prods[pb_]])
                        DV(lambda e, j=j, pb_=pb_: e.tensor_reduce(out=zall.rearrange("p (h j) -> p h j", j=16)[:, :, 15 - j],
                                                          in_=prods[pb_].rearrange("p (h d) -> p h d", d=64), axis=AX.X, op=ALU.add),
                           r=[b_prods[pb_]], w=[b_z])
                    DV(lambda e: e.tensor_tensor(out=zbt.rearrange("p (h j) -> p h j", j=16), in0=zall.rearrange("p (h j) -> p h j", j=16),
                                                 in1=biasb[:].unsqueeze(2).to_broadcast([128, 16, 16]), op=ALU.add), r=[b_z, b_biasb], w=[b_sm])
                    AC(lambda e: e.activation(out=Esb, in_=zbt, func=AF.Exp), r=[b_sm], w=[b_sm])
                    AC(lambda e: e.activation(out=SPs, in_=Esb, func=AF.Ln, bias=c_one[:, 0:1]), r=[b_sm, b_const], w=[b_sm])
                    DV(lambda e: e.tensor_tensor_scan(out=Psc, data0=CT["scanm"][:], data1=SPs, initial=0.0, op0=ALU.mult, op1=ALU.add),
                       r=[b_sm, b_const], w=[b_sm])
                    DV(lambda e: e.tensor_tensor(out=PmS, in0=Psc, in1=SPs, op=ALU.subtract), r=[b_sm], w=[b_sm])
                    PEg([lambda e: e.matmul(PS[5][:, 0:256], lhsT=CT["ntinclf"][:], rhs=SPs, start=True, stop=False),
                         lambda e: e.matmul(PS[5][:, 0:256], lhsT=CT["nonesf"][:], rhs=PmS, start=False, stop=True)],
                        r=[b_sm, b_const], w=[b_ps[5]])
                    DV(lambda e: e.tensor_tensor(out=zbt, in0=zbt, in1=PS[5][:, 0:256], op=ALU.add), r=[b_sm, b_ps[5]], w=[b_sm])
                    AC(lambda e: e.activation(out=wall, in_=zbt, func=AF.Exp), r=[b_sm], w=[b_wall])
                    for j in range(16):
                        kb = kc % NKB
                        pb_ = kc % 2
                        kc += 1
                        n_ = 16 * b + j
                        DM(lambda e, kb=kb, n_=n_: e.indirect_dma_start(
                            out=Kt[kb], out_offset=None, in_=cv, in_offset=bass.IndirectOffsetOnAxis(ap=idx[:, n_:n_ + 1], axis=0)),
                            r=[b_idx], w=[b_Kt[kb]], q="pool")
                        PO(lambda e, kb=kb, j=j, pb_=pb_: e.tensor_tensor(
                            out=wvs[pb_].rearrange("p (h d) -> p h d", d=64), in0=Kt[kb].rearrange("p (h d) -> p h d", d=64),
                            in1=wall.rearrange("p (h j) -> p h j", j=16)[:, :, 15 - j:16 - j].to_broadcast([128, 16, 64]), op=ALU.mult),
                            r=[b_Kt[kb], b_wall], w=[b_prods[pb_]])
                        PEg([(lambda e, mm=mm, j=j, b=b, pb_=pb_: e.matmul(PS[osb][:, NS * mm + b:NS * mm + b + 1], lhsT=wvs[pb_][:, 128 * mm:128 * mm + 128],
                                                                 rhs=CT["onesb"][:], start=(j == 0 and b == 0 and mm == 0), stop=(j == 15), skip_group_check=True))
                             for mm in range(8)], r=[b_prods[pb_], b_const], w=[b_ps[osb]])
                DV(lambda e: e.tensor_tensor(out=g2st[:, 0:8 * NS].rearrange("p (m b) -> p m b", b=NS),
                                             in0=PS[osb][:, 0:8 * NS].rearrange("p (m b) -> p m b", b=NS), in1=zs_s, op=ALU.mult),
                   r=[b_ps[osb], b_zss], w=[b_g2st])
                DM(lambda e: e.dma_start(out=g2scr[:, :, T:TT].rearrange("m p b -> p m b"), in_=g2st[:, 0:8 * NS].rearrange("p (m b) -> p m b", b=NS)),
                   r=[b_g2st], w=b_g2scr)
            arelease(mk_l)
            outproj_postnorm("w_out_b", li, "g_post_b", mi, final=(li == 1))

        for vc in range(nvc):
            xp = xp_a[vc]; xs = xs_a[vc]; cc = cc_a[vc]; st_re = st_re_a[vc]; st_im = st_im_a[vc]; pt = pt_a[vc]
            yp = yp_a[vc]; ys = ys_a[vc]; pssm_re = pssm_re_a[vc]; pssm_im = pssm_im_a[vc]; pk = pk_a[vc]; pv = pv_a[vc]
            sssm_re = sssm_re_a[vc]; sssm_im = sssm_im_a[vc]; sk = sk_a[vc]; sv = sv_a[vc]
            m0 = atop[0]
            for i in range(17):
                j = i % 2
                np_ = 128 if i < 16 else NS
                src = xp[128 * i:128 * i + 128, :] if i < 16 else xs
                dst = xres[128 * i:128 * i + 128, :] if i < 16 else xsres
                DM(lambda e, j=j, np_=np_, src=src: e.dma_start(out=xt[j][0:np_, :], in_=src), w=[b_xt[j]])
                DM(lambda e, j=j, np_=np_, dst=dst: e.dma_start(out=dst, in_=xt[j][0:np_, :]), r=[b_xt[j]], w=[b_xres[i]])
            DM(lambda e, pt=pt: e.dma_start(out=idx[:], in_=pt.partition_broadcast(128)), w=[b_idx])
            DV(lambda e: e.tensor_copy(out=y2[:, 0:256], in_=idx[:]), r=[b_idx], w=[b_y2])
            DV(lambda e: e.tensor_scalar(out=idx[:], in0=y2[:, 0:256], scalar1=128.0, scalar2=CT["iota"][:, 0:1], op0=ALU.mult, op1=ALU.add),
               r=[b_y2, b_const], w=[b_idx])

            csb = aalloc((NS + 1, D)); b_csb = Buf()
            scT = aalloc((128, 8, NS + 1)); b_scT = Buf()
            DM(lambda e, cc=cc: e.dma_start(out=csb, in_=cc), w=[b_csb])
            AC(lambda e: e.activation(out=csb, in_=csb, func=AF.Silu), r=[b_csb], w=[b_csb])
            PEg([(lambda e, k=k: e.matmul(PS[7][:, 17 * k:17 * k + 17], lhsT=csb[:, 128 * k:128 * k + 128],
                                          rhs=identf[0:NS + 1, 0:NS + 1], start=True, stop=True)) for k in range(8)],
                r=[b_csb, b_const], w=[b_ps[7]])
            DV(lambda e: e.tensor_copy(out=scT.rearrange("p k r -> p (k r)"), in_=PS[7][:, 0:8 * 17]), r=[b_ps[7]], w=[b_scT])

            mod_specs = [("w_mod_a", 0, "b_mod_a", 3072), ("w_mod_a", 1, "b_mod_a", 3072), ("w_mod_kv", 0, "b_mod_kv", 2048),
                         ("w_mod_b", 0, "b_mod_b", 3072), ("w_mod_b", 1, "b_mod_b", 3072)]
            HC = 1536
            wmod = [aalloc((128, HC)) for i in range(2)]; b_wmod = [Buf(), Buf()]
            bF = aalloc((128, 24)); b_bF = Buf()
            brow = aalloc((NS + 1, HC)); b_brow = Buf()
            mtok = aalloc((NS + 1, HC)); b_mtok = Buf()
            kk = 0
            for mi, (wn, li, bn, ncol) in enumerate(mod_specs):
                DM(lambda e, bn=bn, li=li, ncol=ncol: e.dma_start(
                    out=bF[:, 0:ncol // 128], in_=Wd[bn][li:li + 1, :].rearrange("o (a p) -> p (o a)", p=128)), w=[b_bF])
                for hf in range((ncol + HC - 1) // HC):
                    h0c = hf * HC
                    hc = min(HC, ncol - h0c)
                    na = hc // 128
                    npc = hc // 512
                    DM(lambda e, bn=bn, li=li, h0c=h0c, hc=hc: e.dma_start(
                        out=brow[:, 0:hc], in_=Wd[bn][li:li + 1, h0c:h0c + hc].partition_broadcast(NS + 1)), w=[b_brow])
                    for k in range(8):
                        j = kk % 2
                        kk += 1
                        DM(lambda e, wn=wn, li=li, k=k, j=j, h0c=h0c, hc=hc: e.dma_start(
                            out=wmod[j][:, 0:hc], in_=Wd[wn][li, 128 * k:128 * k + 128, h0c:h0c + hc]), w=[b_wmod[j]])
                        fns = []
                        for a in range(na):
                            fns.append(lambda e, a=a, k=k, j=j: e.matmul(
                                PS[6][:, 17 * a:17 * a + 17], lhsT=wmod[j][:, 128 * a:128 * a + 128], rhs=scT[:, k, :],
                                start=(k == 0 and a == 0), stop=(k == 7), skip_group_check=True))
                        for pc in range(npc):
                            fns.append(lambda e, pc=pc, k=k, j=j: e.matmul(
                                PS[pc][0:NS + 1, :], lhsT=scT[:, k, :], rhs=wmod[j][:, 512 * pc:512 * pc + 512],
                                start=(k == 0), stop=(k == 7)))
                        PEg(fns, r=[b_wmod[j], b_scT], w=[b_ps[6]] + [b_ps[pc] for pc in range(npc)])
                    a0 = h0c // 128
                    DV(lambda e, mi=mi, na=na, a0=a0: e.tensor_tensor(
                        out=modF[:, mi, a0:a0 + na, :], in0=PS[6][:, 0:17 * na].rearrange("p (a r) -> p a r", r=17),
                        in1=bF[:, a0:a0 + na].unsqueeze(2).to_broadcast([128, na, NS + 1]), op=ALU.add),
                        r=[b_ps[6], b_bF], w=[b_modF[mi]])
                    for pc in range(npc):
                        DV(lambda e, pc=pc: e.tensor_tensor(out=mtok[:, 512 * pc:512 * pc + 512], in0=PS[pc][0:NS + 1, :],
                                                            in1=brow[:, 512 * pc:512 * pc + 512], op=ALU.add),
                           r=[b_ps[pc], b_brow], w=[b_mtok])
                    DM(lambda e, mi=mi, h0c=h0c, hc=hc: e.dma_start(out=modtok[mi, :, h0c:h0c + hc], in_=mtok[:, 0:hc]),
                       r=[b_mtok], w=[b_modtok[mi]])
            arelease(m0)

            import os as _os
            _skip = _os.environ.get('SKIP_S5') == '1'
            if stage >= 1 and not _skip:
                s5_layer(0)
            if stage >= 2 and not _skip:
                s5_layer(1)

            if stage >= 3:
                big2 = aalloc((128, 16, 1024), BF16); b_V = [Buf("V%d" % i) for i in range(16)]
                b_KT = [Buf("KT%d" % i) for i in range(16)]
                mk_kv = atop[0]
                kvo = [aalloc((128, 512)) for i in range(2)]; b_kvo = [Buf(), Buf()]
                kbf = [aalloc((128, 512), BF16) for i in range(2)]; b_kbf = [Buf(), Buf()]
                wkv = aalloc((128, 8, 512), BF16); b_wkv = Buf()
                prenorm("g_kv", 0, 2)
                cnt = 0
                for pc4 in range(4):
                    isk = pc4 < 2
                    for g4 in range(4):
                        wk, wkb = load_w(Wd["w_kv"][0], 512 * pc4 + 128 * g4)
                        PO(lambda e, g4=g4, wk=wk: e.tensor_copy(out=wkv[:, :, 128 * g4:128 * g4 + 128], in_=wk[:]), r=[wkb], w=[b_wkv])
                    gc = 512 * (pc4 % 2)
                    for i in range(17):
                        np_ = 128 if i < 16 else NS
                        c0 = 128 * i if i < 16 else T
                        j = cnt % 2
                        cnt += 1
                        pi = next_ps(0, 4)
                        PEg([(lambda e, k=k, np_=np_, c0=c0, pi=pi: e.matmul(
                            PS[pi][0:np_, :], lhsT=hT[:, k, c0:c0 + np_], rhs=wkv[:, k, :], start=(k == 0), stop=(k == 7)))
                            for k in range(8)], r=[b_wkv, b_hT], w=[b_ps[pi]])
                        AC(lambda e, j=j, np_=np_, pi=pi: e.activation(out=kvo[j][0:np_, :], in_=PS[pi][0:np_, :], func=AF.Identity),
                           r=[b_ps[pi]], w=[b_kvo[j]])
                        if i < 16:
                            dst = (pk if isk else pv)[128 * i:128 * i + 128, gc:gc + 512]
                        else:
                            dst = (sk if isk else sv)[:, gc:gc + 512]
                        DM(lambda e, j=j, np_=np_, dst=dst: e.dma_start(out=dst, in_=kvo[j][0:np_, :]), r=[b_kvo[j]], w=[obuf()])
                        if i < 16:
                            if isk and _os.environ.get('KVV') == '5':
                                pass
                            elif isk:
                                DV(lambda e, j=j, pi=pi: e.tensor_copy(out=kbf[j], in_=kvo[j]), r=[b_kvo[j]], w=[b_kbf[j]])
                                pi2 = next_ps(4, 8)
                                pst = PS[pi2].bitcast(BF16)
                                PEg([(lambda e, j=j, pst=pst, a4=a4: e.transpose(out=pst[:, 128 * a4:128 * a4 + 128], in_=kbf[j][:, 128 * a4:128 * a4 + 128],
                                                                                 identity=identb[:])) for a4 in range(4)],
                                    r=[b_kbf[j], b_const], w=[b_ps[pi2]])
                                DV(lambda e, pc4=pc4, i=i, pst=pst: e.tensor_copy(
                                    out=big1[:, 4 * pc4:4 * pc4 + 4, 128 * i:128 * i + 128], in_=pst[:, 0:512].rearrange("p (a t) -> p a t", a=4)),
                                    r=[b_ps[pi2]], w=[b_KT[i]])
                            else:
                                if _os.environ.get('KVV') == '5':
                                    DV(lambda e, j=j, pi=pi: e.tensor_copy(out=kbf[j], in_=kvo[j]), r=[b_kvo[j]], w=[b_kbf[j]])
                                else:
                                    DV(lambda e, i=i, j=j, gc=gc: e.tensor_copy(out=big2[:, i, gc:gc + 512], in_=kvo[j]),
                                       r=[b_kvo[j]], w=[b_V[i]])

                arelease(mk_kv)

            if stage >= 4:
                sb_layer(0)
            if stage >= 5:
                sb_layer(1)

            if stage < 5:
                for i in range(17):
                    j = i % 2
                    np_ = 128 if i < 16 else NS
                    src = xres[128 * i:128 * i + 128, :] if i < 16 else xsres
                    dst = yp[128 * i:128 * i + 128, :] if i < 16 else ys
                    DM(lambda e, j=j, np_=np_, src=src: e.dma_start(out=xt[j][0:np_, :], in_=src), r=[b_xres[i]], w=[b_xt[j]])
                    DM(lambda e, j=j, np_=np_, dst=dst: e.dma_start(out=dst, in_=xt[j][0:np_, :]), r=[b_xt[j]], w=[obuf()])

            arelease(0)

        S.wait_all("sp", out_bufs)
        S.barrier()
        with nc.allow_non_contiguous_dma("small strided parameter loads"), nc.allow_low_precision("bf16 matmul operands"):
            S.emit(st)
    return nc, consts


NCORES = 8


def make_in_maps(inputs, consts, npg=NPG, ncores=NCORES, groups=8):
    nvc = groups // ncores
    ckf = np.ascontiguousarray(inputs["cache_k"]).reshape(npg * 128, D)
    cvf = np.ascontiguousarray(inputs["cache_v"]).reshape(npg * 128, D)
    maps = []
    for c in range(ncores):
        g0, g1 = c * nvc, (c + 1) * nvc
        m = {}
        m["xp"] = np.ascontiguousarray(inputs["x_prompt"][g0:g1])
        m["xs"] = np.ascontiguousarray(inputs["x_sample"][NS * g0:NS * g1, 0, :]).reshape(nvc, NS, D)
        cs = np.asarray(inputs["c_sample"][NS * g0:NS * g1]).reshape(nvc, NS, D)
        cp = np.asarray(inputs["c_prompt"][g0:g1]).reshape(nvc, 1, D)
        m["cc"] = np.ascontiguousarray(np.concatenate([cp, cs], axis=1))
        for nm, key in (("st_re", "state_ssm_re"), ("st_im", "state_ssm_im")):
            a_ = np.asarray(inputs[key])[:, NS * g0:NS * g1].reshape(2, nvc, NS, 4096)
            m[nm] = np.ascontiguousarray(a_.transpose(1, 0, 2, 3))
        m["ck"] = ckf
        m["cv"] = cvf
        m["pt"] = np.ascontiguousarray(np.asarray(inputs["page_table"])[NS * g0:NS * g1].reshape(nvc, 1, 256)).astype(np.int32)
        for n, s_ in WNAMES:
            m[n] = np.ascontiguousarray(np.asarray(inputs[n], dtype=np.float32).reshape(s_))
        for n, a_ in consts.items():
            m["c_" + n] = a_
        maps.append(m)
    return maps


_CACHE = {}


def kernel(**inputs):
    inputs = {k: np.asarray(v) for k, v in inputs.items()}
    nvc = 8 // NCORES
    if "nc" not in _CACHE:
        _CACHE["nc"] = build(nvc=nvc)
    nc, consts = _CACHE["nc"]
    maps = make_in_maps(inputs, consts)
    res = run_bass_kernel_spmd(nc, maps, core_ids=list(range(NCORES)))
    R = res.results
    cat = lambda k: np.concatenate([R[c][k] for c in range(NCORES)], axis=0)
    y_prompt = cat("yp")
    y_sample = cat("ys").reshape(128, 1, D)
    p_re = cat("pssm_re").reshape(8, 2, 64, 64).transpose(1, 0, 2, 3)
    p_im = cat("pssm_im").reshape(8, 2, 64, 64).transpose(1, 0, 2, 3)
    p_k = cat("pk").reshape(8, T, 16, 64)
    p_v = cat("pv").reshape(8, T, 16, 64)
    s_re = cat("sssm_re").reshape(8, 2, NS, 64, 64).transpose(1, 0, 2, 3, 4).reshape(2, 128, 64, 64)
    s_im = cat("sssm_im").reshape(8, 2, NS, 64, 64).transpose(1, 0, 2, 3, 4).reshape(2, 128, 64, 64)
    s_k = cat("sk").reshape(128, 1, 16, 64)
    s_v = cat("sv").reshape(128, 1, 16, 64)
    f = lambda a: np.ascontiguousarray(a, dtype=np.float32)
    return tuple(f(a) for a in (y_prompt, y_sample, p_re, p_im, p_k, p_v, s_re, s_im, s_k, s_v))
```
